# Optimizing a Trainium2 kernel written in Bass

```python
import jax, jax.numpy as jnp
from jax import lax
import numpy as np

D_MODEL = 2048
BATCH = 4
SEQ = 8192
DEPTH = 1

CHUNK = 64
D_MIX = D_MODEL
D_A = D_MIX // 2
D_B = D_MIX - D_A
GROUP_WIDTH = 128
N_GROUPS_A = D_A // GROUP_WIDTH
N_GROUPS_B = D_B // GROUP_WIDTH
CONV_A_WIDTH = 31
CONV_B_WIDTH = 3
CONV_FFN_WIDTH = 3
D_FF = 5632
PLE_DIM = 256
D_IN_PROJ = 2 * D_A + 3 * D_B
EPS = 1e-6

kernel_name = "hybrid_conformer_shortconv_block"


def rmsnorm(x, g):
    xf = x.astype(jnp.float32)
    y = xf * lax.rsqrt(jnp.mean(xf * xf, axis=-1, keepdims=True) + EPS)
    return (y * g.astype(jnp.float32)).astype(x.dtype)


def layernorm(x, g, b):
    xf = x.astype(jnp.float32)
    mu = jnp.mean(xf, axis=-1, keepdims=True)
    var = jnp.mean(jnp.square(xf - mu), axis=-1, keepdims=True)
    y = (xf - mu) * lax.rsqrt(var + EPS)
    return (y * g.astype(jnp.float32) + b.astype(jnp.float32)).astype(x.dtype)


def causal_dwconv(x, w):
    k, c = w.shape
    return lax.conv_general_dilated(
        x, w[:, None, :].astype(x.dtype),
        window_strides=(1,), padding=[(k - 1, 0)],
        dimension_numbers=("NWC", "WIO", "NWC"),
        feature_group_count=c)


def setup_inputs(seed: int = 0) -> dict:
    key = jax.random.key(seed)
    ks = jax.random.split(key, 20)
    f32 = jnp.float32
    nrm = lambda k, shape, scale: jax.random.normal(k, shape, f32) * scale
    return {
        "x": nrm(ks[0], (BATCH, SEQ, D_MODEL), 1.0),
        "p": nrm(ks[1], (DEPTH, BATCH, SEQ, PLE_DIM), 1.0),
        "norm_mix_g": 1.0 + nrm(ks[2], (DEPTH, D_MODEL), 0.02),
        "w_in": nrm(ks[3], (DEPTH, D_MODEL, D_IN_PROJ), D_MODEL ** -0.5),
        "conv_a_w": nrm(ks[4], (DEPTH, CONV_A_WIDTH, D_A), CONV_A_WIDTH ** -0.5),
        "conv_a_b": nrm(ks[5], (DEPTH, D_A), 0.02),
        "ln_a_g": 1.0 + nrm(ks[6], (DEPTH, D_A), 0.02),
        "ln_a_b": nrm(ks[7], (DEPTH, D_A), 0.02),
        "conv_b_w": nrm(ks[8], (DEPTH, CONV_B_WIDTH, D_B), CONV_B_WIDTH ** -0.5),
        "w_out": nrm(ks[9], (DEPTH, D_MIX, D_MODEL), D_MIX ** -0.5),
        "norm_ffn_g": 1.0 + nrm(ks[10], (DEPTH, D_MODEL), 0.02),
        "w_up": nrm(ks[11], (DEPTH, D_MODEL, 2 * D_FF), D_MODEL ** -0.5),
        "conv_ffn_w": nrm(ks[12], (DEPTH, CONV_FFN_WIDTH, 2 * D_FF), CONV_FFN_WIDTH ** -0.5),
        "w_down": nrm(ks[13], (DEPTH, D_FF, D_MODEL), D_FF ** -0.5),
        "w_ple_gate": nrm(ks[14], (DEPTH, D_MODEL, D_MODEL), D_MODEL ** -0.5),
        "b_ple_gate": nrm(ks[15], (DEPTH, D_MODEL), 0.02),
        "w_ple_proj": nrm(ks[16], (DEPTH, PLE_DIM, D_MODEL), PLE_DIM ** -0.5),
        "norm_final_g": 1.0 + nrm(ks[17], (D_MODEL,), 0.02),
    }


def reference(x, p, norm_mix_g, w_in, conv_a_w, conv_a_b, ln_a_g, ln_a_b, conv_b_w,
              w_out, norm_ffn_g, w_up, conv_ffn_w, w_down, w_ple_gate, b_ple_gate,
              w_ple_proj, norm_final_g):
    h = x
    split_pts = [D_A, 2 * D_A, 2 * D_A + D_B, 2 * D_A + 2 * D_B]
    for i in range(DEPTH):
        hn = rmsnorm(h, norm_mix_g[i])
        z = jnp.einsum("bsd,de->bse", hn, w_in[i])
        a_val, a_gate, b_gate, c_gate, b_h = jnp.split(z, split_pts, axis=-1)
        a = a_val * jax.nn.sigmoid(a_gate)
        a = causal_dwconv(a, conv_a_w[i]) + conv_a_b[i]
        a = jax.nn.silu(layernorm(a, ln_a_g[i], ln_a_b[i]))
        bx = b_gate * causal_dwconv(c_gate * b_h, conv_b_w[i])
        mix = jnp.einsum("bse,ed->bsd", jnp.concatenate([a, bx], axis=-1), w_out[i])
        h = h + mix
        hn = rmsnorm(h, norm_ffn_g[i])
        u = causal_dwconv(jnp.einsum("bsd,df->bsf", hn, w_up[i]), conv_ffn_w[i])
        g, up = jnp.split(u, 2, axis=-1)
        h = h + jnp.einsum("bsf,fd->bsd", jax.nn.silu(g) * up, w_down[i])
        gate = jax.nn.sigmoid(jnp.einsum("bsd,de->bse", h, w_ple_gate[i]) + b_ple_gate[i])
        h = h + jnp.einsum("bsk,kd->bsd", p[i], w_ple_proj[i]) * gate
    return rmsnorm(h, norm_final_g)
```

```python
import numpy as np
import concourse.bass as bass
import concourse.mybir as mybir
from concourse.bass_utils import run_bass_kernel_spmd

F32 = mybir.dt.float32
BF16 = mybir.dt.bfloat16
ALU = mybir.AluOpType
AF = mybir.ActivationFunctionType

NCORES = 8
D = 2048
KD = 16
GA = 8
DFF = 5632
KF = 44
SEQ = 8192
TOK = 4096
HALO = 32
TWM = 512
WIDTHS = [512] + [452] * 8
C0S = [sum(WIDTHS[:i]) for i in range(len(WIDTHS))]
NT_RUN = len(WIDTHS)
EPS = 1e-6
CA = 31

C_G1, C_G2, C_GF, C_CAB, C_LNG, C_LNB, C_CBW, C_CFW, C_BG, C_FLAG, C_EPS, C_CAW = 0, 16, 32, 48, 56, 64, 72, 96, 360, 376, 377, 378
NCONST = C_CAW + GA * CA

NSLOT = 4
SLOT_ELEMS = 4096
DVEG = (0, 1, 2, 3)
TAPS_A = 12
TAPS_B = 5
LN_AT = 99
LN_HALF = 2


class Op:
    __slots__ = ("eng", "fn", "deps", "is_dma", "sem", "sig", "sig_val", "wr", "idx", "inc")

    def __init__(self, eng, fn, is_dma=False, sem=None):
        self.eng = eng
        self.fn = fn
        self.deps = set()
        self.is_dma = is_dma
        self.sem = sem
        self.sig = is_dma
        self.sig_val = None
        self.wr = False
        self.inc = 16 if is_dma else 1


class Sched:
    def __init__(self):
        self.ops = {e: [] for e in ("pe", "act", "dve", "pool", "sp")}
        self.last_w = {}
        self.readers = {}
        self.semcount = {}

    def add(self, eng, fn, reads=(), writes=(), dma_sem=None):
        op = Op(eng, fn, is_dma=dma_sem is not None, sem=dma_sem)
        if dma_sem is not None:
            c = self.semcount.get(id(dma_sem), 0) + 16
            self.semcount[id(dma_sem)] = c
            op.sig_val = c
        deps = set()
        for r in reads:
            w = self.last_w.get(r)
            if w is not None:
                deps.add((w, True))
        for w in writes:
            lw = self.last_w.get(w)
            if lw is not None:
                deps.add((lw, True))
            for rd in self.readers.get(w, ()):
                deps.add((rd, False))
        for r in reads:
            self.readers.setdefault(r, []).append(op)
        for w in writes:
            self.last_w[w] = op
            self.readers[w] = []
        for d, hard in deps:
            if d is op:
                continue
            if (not d.is_dma) and d.eng == eng and not op.is_dma:
                if eng == "pe" or not hard:
                    continue
            op.deps.add(d)
            d.sig = True
        self.ops[eng].append(op)
        return op


def build_program():
    nc = bass.Bass("TRN2", target_bir_lowering=False)
    xT = nc.dram_tensor("xT", [D, HALO + TOK], F32, kind="ExternalInput").ap()
    pT = nc.dram_tensor("pT", [256, HALO + TOK], F32, kind="ExternalInput").ap()
    cst = nc.dram_tensor("cst", [128, NCONST], F32, kind="ExternalInput").ap()
    idn = nc.dram_tensor("idn", [128, 128], F32, kind="ExternalInput").ap()
    wsrc = {
        "inA": nc.dram_tensor("w_inA", [8 * 128, 4096], F32, kind="ExternalInput").ap(),
        "inB": nc.dram_tensor("w_inB", [12 * 128, 4096], F32, kind="ExternalInput").ap(),
        "out": nc.dram_tensor("w_out", [8 * 128, 4096], F32, kind="ExternalInput").ap(),
        "up": nc.dram_tensor("w_up", [44 * 128, 4096], F32, kind="ExternalInput").ap(),
        "down": nc.dram_tensor("w_down", [32 * 128, 2816], F32, kind="ExternalInput").ap(),
        "gate": nc.dram_tensor("w_gate", [8 * 128, 4096], F32, kind="ExternalInput").ap(),
        "ple": nc.dram_tensor("w_ple", [128, 4096], F32, kind="ExternalInput").ap(),
    }
    scr = {k: nc.dram_tensor("s_" + k, list(v.shape), BF16, kind="Internal").ap() for k, v in wsrc.items()}
    scr["diag"] = nc.dram_tensor("s_diag", [GA * 128, CA * 128], BF16, kind="Internal").ap()
    yT = nc.dram_tensor("yT", [D, TOK], F32, kind="ExternalOutput").ap()

    S = Sched()
    from contextlib import ExitStack

    with ExitStack() as es:
        def sb(name, shape, dt):
            return es.enter_context(nc.sbuf_tensor(name, shape, dt))

        def sem(name):
            return es.enter_context(nc.semaphore(name))

        h = [sb("h%d" % i, [128, KD, TWM], F32) for i in range(2)]
        hn = sb("hn", [128, KD, TWM], BF16)
        abf = sb("abf", [128, GA, 30 + TWM], BF16)
        R = sb("R", [128, KF * TWM // 2], F32)
        Rb = R[:, :].bitcast(BF16)
        act = Rb.rearrange("p (f t) -> p f t", t=TWM)
        cat = act
        aconv_t = R[:, 8 * TWM:16 * TWM].rearrange("p (g t) -> p g t", t=TWM)
        ab = Rb[:, 32 * TWM:36 * TWM].rearrange("p (s t) -> p s t", t=TWM)
        asq = Rb[:, 36 * TWM:40 * TWM].rearrange("p (s t) -> p s t", t=TWM)
        hsq = sb("hsq", [128, 2, TWM], BF16)
        pbf = sb("pbf", [128, 2, 2, TWM], BF16)
        T = sb("T", [128, 8, TWM + 2], F32)
        chtail = sb("chtail", [128, GA, 2], F32)
        utail = sb("utail", [128, 2 * KF, 2], F32)
        st_rstd = sb("st_rstd", [128, TWM], F32)
        st_mean = sb("st_mean", [128, TWM], F32)
        st_var = sb("st_var", [128, TWM], F32)
        st_nmr = sb("st_nmr", [128, TWM], F32)
        csb = sb("csb", [128, NCONST], F32)
        ident = sb("ident", [128, 128], F32)
        meanD = sb("meanD", [128, 128], BF16)
        meanA = sb("meanA", [128, 128], BF16)
        plew = sb("plew", [128, 2, D], BF16)
        hnh = sb("hnh", [128, KD, 2], BF16)
        ring = [sb("ring%d" % i, [128, SLOT_ELEMS], BF16) for i in range(NSLOT)]
        h1b = h[1][:, :, :].rearrange("p k t -> p (k t)").bitcast(BF16)
        dstage = [h1b[:, i * CA * 128:(i + 1) * CA * 128].rearrange("p (k m) -> p k m", m=128) for i in range(4)]
        Tb = T[:, :, :].rearrange("p s t -> p (s t)").bitcast(BF16)
        chb = T[:, 6:8, :]
        ps = es.enter_context(nc.psum_tensor("ps", [128, 8, 512], F32))

        esem = {e: sem("e_" + e) for e in ("pe", "act", "dve", "pool")}
        ring_sem = [sem("ring%d" % i) for i in range(NSLOT)]
        x_sem = [sem("x%d" % i) for i in range(2)]
        p_sem = [sem("p%d" % i) for i in range(2)]
        o_sem = [sem("o%d" % i) for i in range(2)]
        c_sem = sem("cst")
        d_sem = [sem("dg%d" % i) for i in range(4)]
        cv_sems = {}

        def col(c):
            return csb[:, c:c + 1]

        def load_x(q):
            hi = q % 2
            c0, TW = C0S[q], WIDTHS[q]
            S.add("act", (lambda e: e.dma_start(out=h[hi][:, :, 0:TW],
                                                in_=xT[:, c0:c0 + TW].rearrange("(k p) t -> p k t", p=128))),
                  writes=[("h", hi, kc) for kc in range(KD)], dma_sem=x_sem[hi])
            S.add("pool", (lambda e: e.dma_start(out=pbf[:, q % 2, :, 0:TW],
                                                 in_=pT[:, c0:c0 + TW].rearrange("(c p) t -> p c t", p=128))),
                  writes=[("pbf", q % 2)], dma_sem=p_sem[q % 2])

        load_x(0)
        i_sem = sem("idn")
        pw_sem = sem("plw")
        S.add("sp", lambda e: e.dma_start(out=csb[:, :], in_=cst), writes=[("csb",)], dma_sem=c_sem)
        S.add("sp", lambda e: e.dma_start(out=ident[:, :], in_=idn), writes=[("ident",)], dma_sem=i_sem)

        wb_sem = [sem("wb%d" % i) for i in range(NSLOT)]
        loaded_once = set()
        S.add("pool", lambda e: e.dma_start(out=plew[:, :, :].rearrange("p c e -> p (c e)"), in_=wsrc["ple"]),
              writes=[("plew",)], dma_sem=pw_sem)

        S.add("dve", lambda e: e.memset(meanD[:, :], 1.0 / D), writes=[("meanD",)])
        S.add("dve", lambda e: e.memset(meanA[:, :], 1.0 / 1024.0), writes=[("meanA",)])
        S.add("dve", lambda e: e.memset(abf[:, :, :], 0.0), writes=[("abf", g) for g in range(GA)])
        S.add("dve", lambda e: e.memset(chtail[:, :, :], 0.0), writes=[("chtail", g) for g in range(GA)])
        S.add("dve", lambda e: e.memset(utail[:, :, :], 0.0), writes=[("utail", s_) for s_ in range(2 * KF)])

        h1keys = [("h", 1, kc) for kc in range(KD)]
        for gi_, g in enumerate([g for g in range(GA) if g not in DVEG]):
            ds = dstage[gi_ % 4]

            def mk(e, g=g, ds=ds):
                ins = None
                for k in range(CA):
                    ins = e.tensor_scalar(ds[:, k, :], ident[:, :], col(C_CAW + g * CA + k), None, ALU.mult)
                return ins
            S.add("dve", mk, reads=[("csb",), ("ident",)], writes=[("dstage", gi_ % 4)])
            S.add("sp", (lambda e, g=g, ds=ds: e.dma_start(out=scr["diag"][g * 128:(g + 1) * 128, :].rearrange("p (k m) -> p k m", m=128),
                                                           in_=ds)),
                  reads=[("dstage", gi_ % 4)] + h1keys, writes=[("scr", "diag", g)], dma_sem=d_sem[gi_ % 4])

        state = {"slab": 0, "bank": 0, "pending_out": None, "slabs_since": 0}

        def load_slab(nm, s_, L):
            n = state["slab"]
            state["slab"] += 1
            slot = n % NSLOT
            rows = scr[nm][s_ * 128:(s_ + 1) * 128, :]
            if nm != "diag" and (nm, s_) not in loaded_once:
                loaded_once.add((nm, s_))
                src = wsrc[nm][s_ * 128:(s_ + 1) * 128, :]
                S.add("pool", (lambda e, slot=slot, src=src, L=L: e.dma_start(out=ring[slot][:, 0:L], in_=src)),
                      writes=[("ring", slot)], dma_sem=ring_sem[slot])
                S.add("sp", (lambda e, slot=slot, rows=rows, L=L: e.dma_start(out=rows, in_=ring[slot][:, 0:L])),
                      reads=[("ring", slot)], writes=[("scr", nm, s_)], dma_sem=wb_sem[slot])
            else:
                S.add("sp", (lambda e, slot=slot, rows=rows, L=L: e.dma_start(out=ring[slot][:, 0:L], in_=rows)),
                      reads=[("scr", nm, s_)], writes=[("ring", slot)], dma_sem=ring_sem[slot])
            state["slabs_since"] += 1
            if state["pending_out"] is not None and state["slabs_since"] >= 3:
                state["pending_out"]()
                state["pending_out"] = None
            return slot

        def next_bank():
            b = state["bank"]
            state["bank"] = (b + 1) % 6
            return b

        def mm_group(bank, TW, pairs, reads, start=True, stop=True, c0=0, writes=None):
            def fn(e, bank=bank, TW=TW, pairs=pairs, start=start, stop=stop, c0=c0):
                ins = None
                n = len(pairs)
                for i_, (l, r) in enumerate(pairs):
                    ins = e.matmul(ps[:, bank, c0:c0 + TW], l, r, start=(start and i_ == 0), stop=(stop and i_ == n - 1))
                return ins
            return S.add("pe", fn, reads=reads, writes=([("ps", bank)] if writes is None else writes))

        def sq_slot_default(kc, TW):
            return hsq[:, kc % 2, 0:TW], ("hsq", kc % 2)

        def sq_slot_act(kc, TW):
            return act[:, 16 + kc, 0:TW], ("act", 16 + kc)

        def sq_slot_T(kc, TW):
            o = (kc // 2) * 2 * (TWM + 2) + (kc % 2) * TWM
            return Tb[:, o:o + TW], ("T", kc // 2)

        def rms_sq(hb, hi, TW, slotf, kcs=range(KD)):
            for kc in kcs:
                ap, key = slotf(kc, TW)
                S.add("act", (lambda e, kc=kc, ap=ap: e.activation(ap, hb[:, kc, 0:TW], AF.Square)),
                      reads=[("h", hi, kc)], writes=[key])

        def rms_stats(TW, slotf, mask_halo=False):
            for kc in range(KD):
                ap, key = slotf(kc, TW)

                def fn(e, kc=kc, ap=ap):
                    return e.matmul(ps[:, 7, 0:TW], meanD[:, :], ap, start=(kc == 0), stop=(kc == KD - 1))
                S.add("pe", fn, reads=[key, ("meanD",)], writes=[("ps", 7)])
            S.add("act", lambda e: e.activation(st_rstd[:, 0:TW], ps[:, 7, 0:TW], AF.Sqrt, bias=col(C_EPS), scale=1.0),
                  reads=[("ps", 7), ("csb",)], writes=[("rstd",)])
            S.add("dve", lambda e: e.reciprocal(st_rstd[:, 0:TW], st_rstd[:, 0:TW]),
                  reads=[("rstd",)], writes=[("rstd",)])
            if mask_halo:
                S.add("dve", lambda e: e.tensor_scalar(st_rstd[:, 0:HALO], st_rstd[:, 0:HALO], col(C_FLAG), None, ALU.mult),
                      reads=[("rstd",), ("csb",)], writes=[("rstd",)])

        def rms_apply(hb, hi, TW, gcol0, dst, dst_key, inplace=False):
            for kc in range(KD):
                S.add("dve", (lambda e, kc=kc: e.scalar_tensor_tensor(dst[:, kc, 0:TW], hb[:, kc, 0:TW], col(gcol0 + kc),
                                                                       st_rstd[:, 0:TW], ALU.mult, ALU.mult)),
                      reads=[("h", hi, kc), ("rstd",), ("csb",)], writes=[("h", hi, kc) if inplace else (dst_key, kc)])

        def rms(hb, hi, TW, gcol0, dst, dst_key, masked=False, inplace=False):
            for kc in range(KD):
                ap, key = sq_slot_default(kc, TW)
                S.add("act", (lambda e, kc=kc, ap=ap: e.activation(ap, hb[:, kc, 0:TW], AF.Square)),
                      reads=[("h", hi, kc)], writes=[key])

                def fn(e, kc=kc, ap=ap):
                    return e.matmul(ps[:, 7, 0:TW], meanD[:, :], ap, start=(kc == 0), stop=(kc == KD - 1))
                S.add("pe", fn, reads=[key, ("meanD",)], writes=[("ps", 7)])
            S.add("act", lambda e: e.activation(st_rstd[:, 0:TW], ps[:, 7, 0:TW], AF.Sqrt, bias=col(C_EPS), scale=1.0),
                  reads=[("ps", 7), ("csb",)], writes=[("rstd",)])
            S.add("dve", lambda e: e.reciprocal(st_rstd[:, 0:TW], st_rstd[:, 0:TW]),
                  reads=[("rstd",)], writes=[("rstd",)])
            rms_apply(hb, hi, TW, gcol0, dst, dst_key, inplace=inplace)

        def aconv_keys(g):
            return [("act", 16 + 2 * g), ("act", 17 + 2 * g)]

        def tile(q):
            hi = q % 2
            hb = h[hi]
            TW = WIDTHS[q]
            i = q
            AUX = "dve" if q == 0 else "pool"
            hn_all = [("hn", kc) for kc in range(KD)]
            if q == 0:
                rms(hb, hi, TW, C_G1, hn, "hn")

            stats_order = [g for g in range(GA) if g not in DVEG] + list(DVEG)
            g_first, g_last = stats_order[0], stats_order[-1]
            tapq = {g: [] for g in DVEG}
            rr = {"i": 0}

            def queue_taps(g):
                tapq[g].append(lambda: S.add("dve", (lambda e: e.tensor_scalar(aconv_t[:, g, 0:TW], abf[:, g, 0:TW], col(C_CAW + g * CA),
                                                                                col(C_CAB + g), ALU.mult, ALU.add)),
                                             reads=[("abf", g), ("csb",)], writes=aconv_keys(g)))
                for k in range(1, CA):
                    tapq[g].append(lambda k=k: S.add("dve", (lambda e: e.scalar_tensor_tensor(aconv_t[:, g, 0:TW], abf[:, g, k:k + TW],
                                                                                               col(C_CAW + g * CA + k),
                                                                                               aconv_t[:, g, 0:TW], ALU.mult, ALU.add)),
                                                     reads=[("abf", g), ("csb",)] + aconv_keys(g), writes=aconv_keys(g)))

            def drain(n):
                gs = [g for g in DVEG]
                while n > 0:
                    live = [g for g in gs if tapq[g]]
                    if not live:
                        return
                    g = live[rr["i"] % len(live)]
                    rr["i"] += 1
                    tapq[g].pop(0)()
                    n -= 1

            def stats_mm(g):
                S.add("pe", (lambda e: e.matmul(ps[:, 7, 0:TW], meanA[:, :], ab[:, g % 4, 0:TW], start=(g == g_first), stop=(g == g_last))),
                      reads=[("act", 32 + g % 4), ("meanA",)], writes=[("ps", 7)])
                S.add("pe", (lambda e: e.matmul(ps[:, 6, 0:TW], meanA[:, :], asq[:, g % 4, 0:TW], start=(g == g_first), stop=(g == g_last))),
                      reads=[("act", 36 + g % 4), ("meanA",)], writes=[("ps", 6)])

            pend_stats = []

            def flush_stats():
                while pend_stats:
                    stats_mm(pend_stats.pop(0))

            def conv_group(g):
                flush_stats()
                dsl = load_slab("diag", g, CA * 128)
                dg = ring[dsl][:, 0:CA * 128].rearrange("p (k m) -> p k m", m=128)
                bC = next_bank()
                mm_group(bC, TW, [(dg[:, k, :], abf[:, g, k:k + TW]) for k in range(CA)],
                         reads=[("ring", dsl), ("abf", g)])
                S.add("act", (lambda e: e.activation(aconv_t[:, g, 0:TW], ps[:, bC, 0:TW], AF.Identity,
                                                     bias=col(C_CAB + g), scale=1.0)),
                      reads=[("ps", bC), ("csb",)], writes=aconv_keys(g))
                S.add("act", (lambda e: e.activation(asq[:, g % 4, 0:TW], ps[:, bC, 0:TW], AF.Square,
                                                     bias=col(C_CAB + g), scale=1.0)),
                      reads=[("ps", bC), ("csb",)], writes=[("act", 36 + g % 4)])
                S.add("act", (lambda e: e.activation(ab[:, g % 4, 0:TW], ps[:, bC, 0:TW], AF.Identity,
                                                     bias=col(C_CAB + g), scale=1.0)),
                      reads=[("ps", bC), ("csb",)], writes=[("act", 32 + g % 4)])
                pend_stats.append(g)

            def sb_group_stats(g):
                S.add("act", (lambda e: e.activation(asq[:, g % 4, 0:TW], aconv_t[:, g, 0:TW], AF.Square)),
                      reads=aconv_keys(g), writes=[("act", 36 + g % 4)])
                S.add("act", (lambda e: e.activation(ab[:, g % 4, 0:TW], aconv_t[:, g, 0:TW], AF.Identity)),
                      reads=aconv_keys(g), writes=[("act", 32 + g % 4)])

            pend = None
            for g in range(GA):
                slot = load_slab("inA", g, 4096)
                rg = ring[slot][:, 0:4096].rearrange("p (k c) -> p k c", c=256)
                bV = next_bank()
                mm_group(bV, TW, [(rg[:, kc, 0:128], hn[:, kc, 0:TW]) for kc in range(KD)], reads=[("ring", slot)] + hn_all)
                bG = next_bank()
                mm_group(bG, TW, [(rg[:, kc, 128:256], hn[:, kc, 0:TW]) for kc in range(KD)], reads=[("ring", slot)] + hn_all)
                ts_ = g % 2
                S.add("act", (lambda e, bG=bG, ts_=ts_: e.activation(T[:, ts_, 0:TW], ps[:, bG, 0:TW], AF.Sigmoid)),
                      reads=[("ps", bG)], writes=[("T", ts_)])
                S.add("dve", (lambda e, bV=bV, ts_=ts_, g=g: e.tensor_tensor(abf[:, g, 30:30 + TW], ps[:, bV, 0:TW],
                                                                             T[:, ts_, 0:TW], ALU.mult)),
                      reads=[("ps", bV), ("T", ts_)], writes=[("abf", g)])
                if g == 0 and state.get("final_tail") is not None:
                    state["final_tail"](0)
                drain(6 if g < 4 else 18)
                if pend is not None:
                    conv_group(pend)
                    pend = None
                if g in DVEG:
                    queue_taps(g)
                else:
                    pend = g
            if pend is not None:
                conv_group(pend)

            applyq = []

            def drain_apply(n):
                while n > 0 and applyq:
                    applyq.pop(0)()
                    n -= 1

            def ln_block():
                flush_stats()
                drain(10 ** 6)
                for g in DVEG:
                    sb_group_stats(g)
                for g in DVEG:
                    stats_mm(g)
                S.add("dve", lambda e: e.tensor_copy(abf[:, :, 0:30], abf[:, :, TW:TW + 30]),
                      reads=[("abf", g) for g in range(GA)], writes=[("abf", g) for g in range(GA)])
                S.add("act", lambda e: e.activation(st_mean[:, 0:TW], ps[:, 7, 0:TW], AF.Identity),
                      reads=[("ps", 7)], writes=[("mean",)])
                S.add("dve", lambda e: e.tensor_tensor(st_nmr[:, 0:TW], st_mean[:, 0:TW], st_mean[:, 0:TW], ALU.mult),
                      reads=[("mean",)], writes=[("nmr",)])
                S.add("dve", lambda e: e.tensor_tensor(st_var[:, 0:TW], ps[:, 6, 0:TW], st_nmr[:, 0:TW], ALU.subtract),
                      reads=[("ps", 6), ("nmr",)], writes=[("var",)])
                S.add("act", lambda e: e.activation(st_var[:, 0:TW], st_var[:, 0:TW], AF.Sqrt, bias=col(C_EPS), scale=1.0),
                      reads=[("var",), ("csb",)], writes=[("var",)])
                S.add("dve", lambda e: e.reciprocal(st_var[:, 0:TW], st_var[:, 0:TW]),
                      reads=[("var",)], writes=[("var",)])
                S.add("dve", lambda e: e.scalar_tensor_tensor(st_nmr[:, 0:TW], st_mean[:, 0:TW], -1.0, st_var[:, 0:TW],
                                                              ALU.mult, ALU.mult),
                      reads=[("mean",), ("var",)], writes=[("nmr",)])
                for g in range(GA):
                    def ap_(g=g):
                        S.add("dve", (lambda e: e.tensor_tensor(aconv_t[:, g, 0:TW], aconv_t[:, g, 0:TW], st_var[:, 0:TW], ALU.mult)),
                              reads=aconv_keys(g) + [("var",)], writes=aconv_keys(g))
                        S.add("dve", (lambda e: e.tensor_tensor(aconv_t[:, g, 0:TW], aconv_t[:, g, 0:TW], st_nmr[:, 0:TW], ALU.add)),
                              reads=aconv_keys(g) + [("nmr",)], writes=aconv_keys(g))
                        S.add("act", (lambda e: e.activation(cat[:, g, 0:TW], aconv_t[:, g, 0:TW], AF.Silu,
                                                             bias=col(C_LNB + g), scale=col(C_LNG + g))),
                              reads=aconv_keys(g) + [("csb",)], writes=[("act", g)])
                    applyq.append(ap_)

            for jp in range(4):
                if jp == LN_AT:
                    ln_block()
                for gi in range(2):
                    g = 2 * jp + gi
                    slot = load_slab("inB", 3 * jp + gi, 4096)
                    rg = ring[slot][:, 0:4096].rearrange("p (k c) -> p k c", c=256)
                    bC = next_bank()
                    mm_group(bC, TW, [(rg[:, kc, 0:128], hn[:, kc, 0:TW]) for kc in range(KD)], reads=[("ring", slot)] + hn_all)
                    bH = next_bank()
                    mm_group(bH, TW, [(rg[:, kc, 128:256], hn[:, kc, 0:TW]) for kc in range(KD)], reads=[("ring", slot)] + hn_all)
                    s2 = g % 2
                    S.add("act", (lambda e, bC=bC, s2=s2: e.activation(T[:, 2 + s2, 0:TW], ps[:, bC, 0:TW], AF.Identity)),
                          reads=[("ps", bC)], writes=[("T", 2 + s2)])
                    S.add("dve", (lambda e, g=g, s2=s2: e.tensor_copy(chb[:, s2, 0:2], chtail[:, g, :])),
                          reads=[("chtail", g)], writes=[("T", 6 + s2)])
                    S.add("dve", (lambda e, bH=bH, s2=s2: e.tensor_tensor(chb[:, s2, 2:2 + TW], ps[:, bH, 0:TW], T[:, 2 + s2, 0:TW], ALU.mult)),
                          reads=[("ps", bH), ("T", 2 + s2)], writes=[("T", 6 + s2)])
                    S.add("dve", (lambda e, g=g, s2=s2: e.tensor_copy(chtail[:, g, :], chb[:, s2, TW:TW + 2])),
                          reads=[("T", 6 + s2)], writes=[("chtail", g)])
                    S.add("dve", (lambda e, g=g, s2=s2: e.tensor_scalar(T[:, 4 + s2, 0:TW], chb[:, s2, 2:2 + TW], col(C_CBW + g * 3 + 2), None, ALU.mult)),
                          reads=[("T", 6 + s2), ("csb",)], writes=[("T", 4 + s2)])
                    for k in (1, 0):
                        S.add("dve", (lambda e, g=g, s2=s2, k=k: e.scalar_tensor_tensor(T[:, 4 + s2, 0:TW], chb[:, s2, k:k + TW],
                                                                                         col(C_CBW + g * 3 + k), T[:, 4 + s2, 0:TW],
                                                                                         ALU.mult, ALU.add)),
                              reads=[("T", 6 + s2), ("csb",), ("T", 4 + s2)], writes=[("T", 4 + s2)])
                    drain(TAPS_B)
                    drain_apply(3)
                if jp == LN_HALF:
                    ln_block()
                slot = load_slab("inB", 3 * jp + 2, 4096)
                rg = ring[slot][:, 0:4096].rearrange("p (k c) -> p k c", c=256)
                for gi in range(2):
                    g = 2 * jp + gi
                    s2 = g % 2
                    bB = next_bank()
                    mm_group(bB, TW, [(rg[:, kc, gi * 128:(gi + 1) * 128], hn[:, kc, 0:TW]) for kc in range(KD)], reads=[("ring", slot)] + hn_all)
                    S.add("dve", (lambda e, g=g, s2=s2, bB=bB: e.tensor_tensor(cat[:, GA + g, 0:TW], T[:, 4 + s2, 0:TW], ps[:, bB, 0:TW], ALU.mult)),
                          reads=[("T", 4 + s2), ("ps", bB)], writes=[("act", GA + g)])
                    drain(TAPS_B // 2)
                    drain_apply(3)
            drain_apply(100)
            cat_all = [("act", c) for c in range(KD)]
            for s_ in range(8):
                if 1 <= s_ <= 3 and state.get("final_tail") is not None:
                    state["final_tail"](s_)
                    if s_ == 3:
                        state["final_tail"] = None
                        state["slabs_since"] = 0
                slot = load_slab("out", s_, 4096)
                rg = ring[slot][:, 0:4096].rearrange("p (k c) -> p k c", c=256)
                for t_ in range(2):
                    m = 2 * s_ + t_
                    b = next_bank()
                    if s_ == 0:
                        mm_group(b, TW, [(rg[:, kc, t_ * 128:(t_ + 1) * 128], cat[:, kc, 0:TW]) for kc in range(KD - 2)],
                                 reads=[("ring", slot)] + cat_all[:KD - 2], stop=False)
                        mm_group(b, TW, [(rg[:, kc, t_ * 128:(t_ + 1) * 128], cat[:, kc, 0:TW]) for kc in range(KD - 2, KD)],
                                 reads=[("ring", slot)] + cat_all[KD - 2:], start=False)
                    else:
                        mm_group(b, TW, [(rg[:, kc, t_ * 128:(t_ + 1) * 128], cat[:, kc, 0:TW]) for kc in range(KD)],
                                 reads=[("ring", slot)] + cat_all)
                    S.add("dve", (lambda e, m=m, b=b: e.tensor_tensor(hb[:, m, 0:TW], hb[:, m, 0:TW], ps[:, b, 0:TW], ALU.add)),
                          reads=[("h", hi, m), ("ps", b)], writes=[("h", hi, m)])
                    S.add("act", (lambda e, m=m: e.activation(hn[:, m, 0:TW], hb[:, m, 0:TW], AF.Identity, scale=col(C_G2 + m))),
                          reads=[("h", hi, m), ("csb",)], writes=[("hn", m)])
                    rms_sq(hb, hi, TW, sq_slot_T, kcs=[m])
            rms_stats(TW, sq_slot_T, mask_halo=(q == 0))
            if q + 1 < NT_RUN:
                load_x(q + 1)
            def ffn_restore(j):
                par = j % 2
                for w_ in range(2):
                    sl = 2 * j + w_
                    u = par * 2 + w_
                    S.add(AUX, (lambda e, u=u, sl=sl: e.tensor_copy(T[:, u, 0:2], utail[:, sl, :])),
                          reads=[("utail", sl)], writes=[("T", u)])

            ffn_restore(0)
            for j in range(KF):
                slot = load_slab("up", j, 4096)
                rg = ring[slot][:, 0:4096].rearrange("p (k c) -> p k c", c=256)
                par = j % 2
                banks = []
                for w_ in range(2):
                    b = next_bank()
                    banks.append(b)
                    mm_group(b, TW, [(rg[:, kc, w_ * 128:(w_ + 1) * 128], hn[:, kc, 0:TW]) for kc in range(KD)],
                             reads=[("ring", slot)] + hn_all)
                for w_ in range(2):
                    sl = 2 * j + w_
                    u = par * 2 + w_
                    b = banks[w_]
                    S.add("dve", (lambda e, u=u, b=b: e.tensor_tensor(T[:, u, 2:2 + TW], ps[:, b, 0:TW], st_rstd[:, 0:TW], ALU.mult)),
                          reads=[("ps", b), ("rstd",)], writes=[("T", u)])
                    S.add(AUX, (lambda e, u=u, sl=sl: e.tensor_copy(utail[:, sl, :], T[:, u, TW:TW + 2])),
                          reads=[("T", u)], writes=[("utail", sl)])
                    S.add("act", (lambda e, u=u, sl=sl: e.activation(T[:, 4 + u, 0:TW], T[:, u, 2:2 + TW], AF.Identity,
                                                                     scale=col(C_CFW + sl * 3 + 2))),
                          reads=[("T", u), ("csb",)], writes=[("T", 4 + u)])
                if j + 1 < KF:
                    ffn_restore(j + 1)
                for w_ in range(2):
                    sl = 2 * j + w_
                    u = par * 2 + w_
                    for k in (1, 0):
                        S.add("dve", (lambda e, u=u, sl=sl, k=k: e.scalar_tensor_tensor(T[:, 4 + u, 0:TW], T[:, u, k:k + TW],
                                                                                         col(C_CFW + sl * 3 + k), T[:, 4 + u, 0:TW],
                                                                                         ALU.mult, ALU.add)),
                              reads=[("T", u), ("T", u), ("csb",), ("T", 4 + u)], writes=[("T", 4 + u)])
                ug, uu = par * 2, par * 2 + 1
                S.add("act", (lambda e, ug=ug: e.activation(T[:, 4 + ug, 0:TW], T[:, 4 + ug, 0:TW], AF.Silu)),
                      reads=[("T", 4 + ug)], writes=[("T", 4 + ug)])
                S.add(AUX, (lambda e, ug=ug, uu=uu, j=j: e.tensor_tensor(act[:, j, 0:TW], T[:, 4 + ug, 0:TW], T[:, 4 + uu, 0:TW], ALU.mult)),
                      reads=[("T", 4 + ug), ("T", 4 + uu)], writes=[("act", j)])
            nxt = q + 1 < NT_RUN
            if nxt:
                hbn, hin, TWn = h[(q + 1) % 2], (q + 1) % 2, WIDTHS[q + 1]

                def sq_slot_hn(kc, TW_):
                    return hn[:, kc, 0:TW_], ("hn", kc)
                rms_sq(hbn, hin, TWn, sq_slot_hn)

            def down_half(m, hf, b):
                slot = load_slab("down", 2 * m + hf, 2816)
                rg = ring[slot][:, 0:2816].rearrange("p (f c) -> p f c", c=128)
                mm_group(b, TW, [(rg[:, f, :], act[:, hf * 22 + f, 0:TW]) for f in range(22)],
                         reads=[("ring", slot)] + [("act", hf * 22 + f) for f in range(22)],
                         start=(hf == 0), stop=(hf == 1))

            def hbf_slot(kc):
                o = (kc // 2) * 2 * (TWM + 2) + (kc % 2) * TWM
                return Tb[:, o:o + TW], ("T", kc // 2)

            def down_evac(m, b):
                S.add("dve", (lambda e, m=m, b=b: e.tensor_tensor(hb[:, m, 0:TW], hb[:, m, 0:TW], ps[:, b, 0:TW], ALU.add)),
                      reads=[("h", hi, m), ("ps", b)], writes=[("h", hi, m)])
                ap, key = hbf_slot(m)
                S.add("act", (lambda e, m=m, ap=ap: e.activation(ap, hb[:, m, 0:TW], AF.Identity)),
                      reads=[("h", hi, m)], writes=[key])

            b3 = [next_bank() for _ in range(3)]
            for m in range(3):
                down_half(m, 0, b3[m])
            for m in range(3):
                down_half(m, 1, b3[m])
                down_evac(m, b3[m])
            for m in range(3, KD):
                b = next_bank()
                down_half(m, 0, b)
                down_half(m, 1, b)
                down_evac(m, b)
            if nxt:
                rms_stats(TWn, sq_slot_hn)
            hbf_all = [("T", k) for k in range(8)]
            hsq32 = hsq[:, :, :].rearrange("p s t -> p (s t)").bitcast(F32)
            gsb_buf = [(st_mean, [("mean",)]), (st_var, [("var",)])]
            tpl_buf = [(st_nmr, [("nmr",)]), (hsq32, [("hsq", 0), ("hsq", 1)])]
            apply_q = []
            for s_ in range(8):
                slot = load_slab("gate", s_, 4096)
                rg = ring[slot][:, 0:4096].rearrange("p (k c) -> p k c", c=256)
                for t_ in range(2):
                    m = 2 * s_ + t_
                    bG = next_bank()
                    mm_group(bG, TW, [(rg[:, kc, t_ * 128:(t_ + 1) * 128], hbf_slot(kc)[0]) for kc in range(KD)],
                             reads=[("ring", slot)] + hbf_all)
                    bP = next_bank()
                    mm_group(bP, TW, [(plew[:, c, m * 128:(m + 1) * 128], pbf[:, i % 2, c, 0:TW]) for c in range(2)],
                             reads=[("plew",), ("pbf", i % 2)])
                    tg = m % 2
                    gsb, gk = gsb_buf[tg]
                    tpl, tk = tpl_buf[tg]
                    S.add("act", (lambda e, bG=bG, gsb=gsb, m=m: e.activation(gsb[:, 0:TW], ps[:, bG, 0:TW], AF.Sigmoid,
                                                                              bias=col(C_BG + m), scale=1.0)),
                          reads=[("ps", bG), ("csb",)], writes=gk)
                    S.add("dve", (lambda e, bP=bP, gsb=gsb, tpl=tpl: e.tensor_tensor(tpl[:, 0:TW], ps[:, bP, 0:TW], gsb[:, 0:TW], ALU.mult)),
                          reads=[("ps", bP)] + gk, writes=tk)
                    S.add(AUX, (lambda e, m=m, tpl=tpl: e.tensor_tensor(hb[:, m, 0:TW], hb[:, m, 0:TW], tpl[:, 0:TW], ALU.add)),
                          reads=[("h", hi, m)] + tk, writes=[("h", hi, m)])
                    rms_sq(hb, hi, TW, sq_slot_act, kcs=[m])
                    if nxt:
                        if 2 <= m <= 9:
                            for kc in (2 * (m - 2), 2 * (m - 2) + 1):
                                S.add("dve", (lambda e, kc=kc: e.scalar_tensor_tensor(hn[:, kc, 0:TWn], hbn[:, kc, 0:TWn], col(C_G1 + kc),
                                                                                       st_rstd[:, 0:TWn], ALU.mult, ALU.mult)),
                                      reads=[("h", hin, kc), ("rstd",), ("csb",)], writes=[("hn", kc)])
            def final_tail(part=None):
                if part is None or part == 0:
                    rms_stats(TW, sq_slot_act)
                kcs = range(KD) if part is None else range(4 * part, 4 * part + 4)
                for kc in kcs:
                    S.add("dve", (lambda e, kc=kc: e.scalar_tensor_tensor(hb[:, kc, 0:TW], hb[:, kc, 0:TW], col(C_GF + kc),
                                                                           st_rstd[:, 0:TW], ALU.mult, ALU.mult)),
                          reads=[("h", hi, kc), ("rstd",), ("csb",)], writes=[("h", hi, kc)])

            def emit_out():
                lo = HALO if q == 0 else 0
                y0 = C0S[q] + lo - HALO
                S.add("act", (lambda e: e.dma_start(out=yT[:, y0:y0 + TW - lo].rearrange("(k p) t -> p k t", p=128),
                                                    in_=hb[:, :, lo:TW])),
                      reads=[("h", hi, kc) for kc in range(KD)], dma_sem=o_sem[hi])
            if q == NT_RUN - 1:
                lo_ = HALO if q == 0 else 0
                y0_ = C0S[q] + lo_ - HALO
                for part in range(4):
                    final_tail(part)
                    S.add("act", (lambda e, part=part: e.dma_start(
                        out=yT[4 * part * 128:(4 * part + 4) * 128, y0_:y0_ + TW - lo_].rearrange("(k p) t -> p k t", p=128),
                        in_=hb[:, 4 * part:4 * part + 4, lo_:TW])),
                          reads=[("h", hi, kc) for kc in range(4 * part, 4 * part + 4)], dma_sem=o_sem[hi])
            else:
                state["final_tail"] = final_tail
                state["pending_out"] = emit_out
                state["slabs_since"] = -10 ** 6

        for q in range(NT_RUN):
            tile(q)
        if state["pending_out"] is not None:
            state["pending_out"]()
            state["pending_out"] = None

        sem_of = {}
        for e, lst in S.ops.items():
            cnt = 0
            for op in lst:
                if op.is_dma:
                    sem_of[id(op.sem)] = [op.sem, S.semcount[id(op.sem)]]
                elif op.sig:
                    cnt += 1
                    op.sig_val = cnt

        def emit(engname, eng):
            waited = {}
            for op in S.ops[engname]:
                need = {}
                for d in op.deps:
                    sm = d.sem if d.is_dma else esem[d.eng]
                    k = id(sm)
                    if k not in need or need[k][1] < d.sig_val:
                        need[k] = (sm, d.sig_val)
                for k, (sm, v) in need.items():
                    if waited.get(k, 0) >= v:
                        continue
                    eng.wait_ge(sm, v)
                    waited[k] = v
                ins = op.fn(eng)
                if op.is_dma:
                    ins.then_inc(op.sem, 16)
                elif op.sig:
                    ins.then_inc(esem[engname], 1)
            return waited

        with nc.Block() as block:
            @block.tensor
            def _(e):
                emit("pe", e)

            @block.scalar
            def _(e):
                emit("act", e)

            @block.vector
            def _(e):
                emit("dve", e)

            @block.gpsimd
            def _(e):
                emit("pool", e)

            @block.sync
            def _(e):
                emit("sp", e)
                for sm in o_sem:
                    c = sem_of.get(id(sm))
                    if c:
                        e.wait_ge(sm, c[1])
    return nc


def _slab(w, cols):
    K = w.shape[0] // 128
    sub = w[:, cols].reshape(K, 128, len(cols)).transpose(1, 0, 2)
    return np.ascontiguousarray(sub).reshape(128, K * len(cols))


def _prep_shared(inp):
    w_in = inp["w_in"][0]
    w_out = inp["w_out"][0]
    w_up = inp["w_up"][0]
    w_down = inp["w_down"][0]
    w_gate = inp["w_ple_gate"][0]
    w_ple = inp["w_ple_proj"][0]
    r128 = np.arange(128)
    inA = [_slab(w_in, np.concatenate([g * 128 + r128, 1024 + g * 128 + r128])) for g in range(8)]
    inB = []
    for jp in range(4):
        g0, g1 = 2 * jp, 2 * jp + 1
        inB.append(_slab(w_in, np.concatenate([3072 + g0 * 128 + r128, 4096 + g0 * 128 + r128])))
        inB.append(_slab(w_in, np.concatenate([3072 + g1 * 128 + r128, 4096 + g1 * 128 + r128])))
        inB.append(_slab(w_in, np.concatenate([2048 + g0 * 128 + r128, 2048 + g1 * 128 + r128])))
    outs = [_slab(w_out, np.arange(s * 256, (s + 1) * 256)) for s in range(8)]
    ups = [_slab(w_up, np.concatenate([j * 128 + r128, DFF + j * 128 + r128])) for j in range(KF)]
    downs = []
    for m in range(16):
        for hf in range(2):
            downs.append(_slab(w_down[hf * 22 * 128:(hf + 1) * 22 * 128, :], np.arange(m * 128, (m + 1) * 128)))
    gates = [_slab(w_gate, np.arange(s * 256, (s + 1) * 256)) for s in range(8)]
    ple = _slab(w_ple, np.arange(2048))
    sh = {
        "w_inA": np.concatenate(inA, 0), "w_inB": np.concatenate(inB, 0), "w_out": np.concatenate(outs, 0),
        "w_up": np.concatenate(ups, 0), "w_down": np.concatenate(downs, 0), "w_gate": np.concatenate(gates, 0),
        "w_ple": ple, "idn": np.eye(128, dtype=np.float32),
    }
    c = np.zeros((128, NCONST), np.float32)

    def chunked(v, n):
        return np.asarray(v).reshape(n, 128).T
    c[:, C_G1:C_G1 + 16] = chunked(inp["norm_mix_g"][0], 16)
    c[:, C_G2:C_G2 + 16] = chunked(inp["norm_ffn_g"][0], 16)
    c[:, C_GF:C_GF + 16] = chunked(inp["norm_final_g"], 16)
    c[:, C_CAB:C_CAB + 8] = chunked(inp["conv_a_b"][0], 8)
    c[:, C_LNG:C_LNG + 8] = chunked(inp["ln_a_g"][0], 8)
    c[:, C_LNB:C_LNB + 8] = chunked(inp["ln_a_b"][0], 8)
    cbw = inp["conv_b_w"][0]
    for g in range(8):
        for k in range(3):
            c[:, C_CBW + g * 3 + k] = cbw[k, g * 128:(g + 1) * 128]
    cfw = inp["conv_ffn_w"][0]
    for j in range(KF):
        for w_ in range(2):
            sl = 2 * j + w_
            f0 = (DFF if w_ else 0) + j * 128
            for k in range(3):
                c[:, C_CFW + sl * 3 + k] = cfw[k, f0:f0 + 128]
    c[:, C_BG:C_BG + 16] = chunked(inp["b_ple_gate"][0], 16)
    caw = inp["conv_a_w"][0]
    for g in range(8):
        for k in range(CA):
            c[:, C_CAW + g * CA + k] = caw[k, g * 128:(g + 1) * 128]
    return sh, c


_NC_CACHE = {}


def kernel(**inputs):
    inp = {k: np.asarray(v) for k, v in inputs.items()}
    x = inp["x"]
    p = inp["p"][0]
    sh, cbase = _prep_shared(inp)
    in_maps = []
    for c in range(NCORES):
        b, half = c // 2, c % 2
        t0 = half * TOK
        xt = np.zeros((D, HALO + TOK), np.float32)
        xt[:, HALO:] = x[b, t0:t0 + TOK, :].T
        if half == 1:
            xt[:, :HALO] = x[b, t0 - HALO:t0, :].T
        cst = cbase.copy()
        cst[:, C_FLAG] = 1.0 if half == 1 else 0.0
        cst[:, C_EPS] = EPS
        m = dict(sh)
        m["xT"] = xt
        pt = np.zeros((256, HALO + TOK), np.float32)
        pt[:, HALO:] = p[b, t0:t0 + TOK, :].T
        m["pT"] = pt
        m["cst"] = cst
        in_maps.append(m)
    if "nc" not in _NC_CACHE:
        _NC_CACHE["nc"] = build_program()
    nc = _NC_CACHE["nc"]
    res = run_bass_kernel_spmd(nc, in_maps, core_ids=list(range(NCORES)))
    out = np.empty((4, SEQ, D), np.float32)
    for c in range(NCORES):
        b, half = c // 2, c % 2
        out[b, half * TOK:(half + 1) * TOK, :] = np.asarray(res.results[c]["yT"]).T
    return out
```

```python
import numpy as np
import concourse.bass as bass
import concourse.mybir as mybir
from concourse.bass_utils import run_bass_kernel_spmd

F32 = mybir.dt.float32
BF16 = mybir.dt.bfloat16
ALU = mybir.AluOpType
AF = mybir.ActivationFunctionType

NCORES = 8
D = 2048
KD = 16
GA = 8
DFF = 5632
KF = 44
SEQ = 8192
TOK = 4096
HALO = 32
TWM = 512
WIDTHS = [512] + [452] * 8
C0S = [sum(WIDTHS[:i]) for i in range(len(WIDTHS))]
NT_RUN = len(WIDTHS)
EPS = 1e-6
CA = 31

C_G1, C_G2, C_GF, C_CAB, C_LNG, C_LNB, C_CBW, C_CFW, C_BG, C_FLAG, C_EPS, C_CAW = 0, 16, 32, 48, 56, 64, 72, 96, 360, 376, 377, 378
NCONST = C_CAW + GA * CA

NSLOT = 4
SLOT_ELEMS = 4096
DVEG = (0, 1, 2, 3)
TAPS_A = 12
TAPS_B = 6
LN_AT = 99
LN_HALF = 2


class Op:
    __slots__ = ("eng", "fn", "deps", "is_dma", "sem", "sig", "sig_val", "wr", "idx", "inc")

    def __init__(self, eng, fn, is_dma=False, sem=None):
        self.eng = eng
        self.fn = fn
        self.deps = set()
        self.is_dma = is_dma
        self.sem = sem
        self.sig = is_dma
        self.sig_val = None
        self.wr = False
        self.inc = 16 if is_dma else 1


class Sched:
    def __init__(self):
        self.ops = {e: [] for e in ("pe", "act", "dve", "pool", "sp")}
        self.last_w = {}
        self.readers = {}
        self.semcount = {}

    def add(self, eng, fn, reads=(), writes=(), dma_sem=None):
        op = Op(eng, fn, is_dma=dma_sem is not None, sem=dma_sem)
        if dma_sem is not None:
            c = self.semcount.get(id(dma_sem), 0) + 16
            self.semcount[id(dma_sem)] = c
            op.sig_val = c
        deps = set()
        for r in reads:
            w = self.last_w.get(r)
            if w is not None:
                deps.add((w, True))
        for w in writes:
            lw = self.last_w.get(w)
            if lw is not None:
                deps.add((lw, True))
            for rd in self.readers.get(w, ()):
                deps.add((rd, False))
        for r in reads:
            self.readers.setdefault(r, []).append(op)
        for w in writes:
            self.last_w[w] = op
            self.readers[w] = []
        for d, hard in deps:
            if d is op:
                continue
            if (not d.is_dma) and d.eng == eng and not op.is_dma:
                if eng == "pe" or not hard:
                    continue
            op.deps.add(d)
            d.sig = True
        self.ops[eng].append(op)
        return op


def build_program():
    nc = bass.Bass("TRN2", target_bir_lowering=False)
    xT = nc.dram_tensor("xT", [D, HALO + TOK], F32, kind="ExternalInput").ap()
    pT = nc.dram_tensor("pT", [256, HALO + TOK], F32, kind="ExternalInput").ap()
    cst = nc.dram_tensor("cst", [128, NCONST], F32, kind="ExternalInput").ap()
    idn = nc.dram_tensor("idn", [128, 128], F32, kind="ExternalInput").ap()
    wsrc = {
        "inA": nc.dram_tensor("w_inA", [8 * 128, 4096], F32, kind="ExternalInput").ap(),
        "inB": nc.dram_tensor("w_inB", [12 * 128, 4096], F32, kind="ExternalInput").ap(),
        "out": nc.dram_tensor("w_out", [8 * 128, 4096], F32, kind="ExternalInput").ap(),
        "up": nc.dram_tensor("w_up", [44 * 128, 4096], F32, kind="ExternalInput").ap(),
        "down": nc.dram_tensor("w_down", [32 * 128, 2816], F32, kind="ExternalInput").ap(),
        "gate": nc.dram_tensor("w_gate", [8 * 128, 4096], F32, kind="ExternalInput").ap(),
        "ple": nc.dram_tensor("w_ple", [128, 4096], F32, kind="ExternalInput").ap(),
    }
    scr = {k: nc.dram_tensor("s_" + k, list(v.shape), BF16, kind="Internal").ap() for k, v in wsrc.items()}
    scr["diag"] = nc.dram_tensor("s_diag", [GA * 128, CA * 128], BF16, kind="Internal").ap()
    yT = nc.dram_tensor("yT", [D, TOK], F32, kind="ExternalOutput").ap()

    S = Sched()
    from contextlib import ExitStack

    with ExitStack() as es:
        def sb(name, shape, dt):
            return es.enter_context(nc.sbuf_tensor(name, shape, dt))

        def sem(name):
            return es.enter_context(nc.semaphore(name))

        h = [sb("h%d" % i, [128, KD, TWM], F32) for i in range(2)]
        hn = sb("hn", [128, KD, TWM], BF16)
        abf = sb("abf", [128, GA, 30 + TWM], BF16)
        R = sb("R", [128, KF * TWM // 2], F32)
        Rb = R[:, :].bitcast(BF16)
        act = Rb.rearrange("p (f t) -> p f t", t=TWM)
        cat = act
        aconv_t = R[:, 8 * TWM:16 * TWM].rearrange("p (g t) -> p g t", t=TWM)
        ab = Rb[:, 32 * TWM:36 * TWM].rearrange("p (s t) -> p s t", t=TWM)
        asq = Rb[:, 36 * TWM:40 * TWM].rearrange("p (s t) -> p s t", t=TWM)
        hsq = sb("hsq", [128, 2, TWM], BF16)
        pbf = sb("pbf", [128, 2, 2, TWM], BF16)
        T = sb("T", [128, 8, TWM + 2], F32)
        chtail = sb("chtail", [128, GA, 2], F32)
        utail = sb("utail", [128, 2 * KF, 2], F32)
        st_rstd = sb("st_rstd", [128, TWM], F32)
        st_mean = sb("st_mean", [128, TWM], F32)
        st_var = sb("st_var", [128, TWM], F32)
        st_nmr = sb("st_nmr", [128, TWM], F32)
        csb = sb("csb", [128, NCONST], F32)
        ident = sb("ident", [128, 128], F32)
        meanD = sb("meanD", [128, 128], BF16)
        meanA = sb("meanA", [128, 128], BF16)
        plew = sb("plew", [128, 2, D], BF16)
        hnh = sb("hnh", [128, KD, 2], BF16)
        ring = [sb("ring%d" % i, [128, SLOT_ELEMS], BF16) for i in range(NSLOT)]
        h1b = h[1][:, :, :].rearrange("p k t -> p (k t)").bitcast(BF16)
        dstage = [h1b[:, i * CA * 128:(i + 1) * CA * 128].rearrange("p (k m) -> p k m", m=128) for i in range(4)]
        Tb = T[:, :, :].rearrange("p s t -> p (s t)").bitcast(BF16)
        chb = T[:, 6:8, :]
        ps = es.enter_context(nc.psum_tensor("ps", [128, 8, 512], F32))

        esem = {e: sem("e_" + e) for e in ("pe", "act", "dve", "pool")}
        ring_sem = [sem("ring%d" % i) for i in range(NSLOT)]
        x_sem = [sem("x%d" % i) for i in range(2)]
        p_sem = [sem("p%d" % i) for i in range(2)]
        o_sem = [sem("o%d" % i) for i in range(2)]
        c_sem = sem("cst")
        d_sem = [sem("dg%d" % i) for i in range(4)]
        cv_sems = {}

        def col(c):
            return csb[:, c:c + 1]

        def load_x(q):
            hi = q % 2
            c0, TW = C0S[q], WIDTHS[q]
            S.add("act", (lambda e: e.dma_start(out=h[hi][:, :, 0:TW],
                                                in_=xT[:, c0:c0 + TW].rearrange("(k p) t -> p k t", p=128))),
                  writes=[("h", hi, kc) for kc in range(KD)], dma_sem=x_sem[hi])
            S.add("pool", (lambda e: e.dma_start(out=pbf[:, q % 2, :, 0:TW],
                                                 in_=pT[:, c0:c0 + TW].rearrange("(c p) t -> p c t", p=128))),
                  writes=[("pbf", q % 2)], dma_sem=p_sem[q % 2])

        load_x(0)
        i_sem = sem("idn")
        pw_sem = sem("plw")
        S.add("sp", lambda e: e.dma_start(out=csb[:, :], in_=cst), writes=[("csb",)], dma_sem=c_sem)
        S.add("sp", lambda e: e.dma_start(out=ident[:, :], in_=idn), writes=[("ident",)], dma_sem=i_sem)

        wb_sem = [sem("wb%d" % i) for i in range(NSLOT)]
        loaded_once = set()
        S.add("pool", lambda e: e.dma_start(out=plew[:, :, :].rearrange("p c e -> p (c e)"), in_=wsrc["ple"]),
              writes=[("plew",)], dma_sem=pw_sem)

        S.add("dve", lambda e: e.memset(meanD[:, :], 1.0 / D), writes=[("meanD",)])
        S.add("dve", lambda e: e.memset(meanA[:, :], 1.0 / 1024.0), writes=[("meanA",)])
        S.add("dve", lambda e: e.memset(abf[:, :, :], 0.0), writes=[("abf", g) for g in range(GA)])
        S.add("dve", lambda e: e.memset(chtail[:, :, :], 0.0), writes=[("chtail", g) for g in range(GA)])
        S.add("dve", lambda e: e.memset(utail[:, :, :], 0.0), writes=[("utail", s_) for s_ in range(2 * KF)])

        h1keys = [("h", 1, kc) for kc in range(KD)]
        for gi_, g in enumerate([g for g in range(GA) if g not in DVEG]):
            ds = dstage[gi_ % 4]

            def mk(e, g=g, ds=ds):
                ins = None
                for k in range(CA):
                    ins = e.tensor_scalar(ds[:, k, :], ident[:, :], col(C_CAW + g * CA + k), None, ALU.mult)
                return ins
            S.add("dve", mk, reads=[("csb",), ("ident",)], writes=[("dstage", gi_ % 4)])
            S.add("sp", (lambda e, g=g, ds=ds: e.dma_start(out=scr["diag"][g * 128:(g + 1) * 128, :].rearrange("p (k m) -> p k m", m=128),
                                                           in_=ds)),
                  reads=[("dstage", gi_ % 4)] + h1keys, writes=[("scr", "diag", g)], dma_sem=d_sem[gi_ % 4])

        state = {"slab": 0, "bank": 0, "pending_out": None, "slabs_since": 0}

        def load_slab(nm, s_, L):
            n = state["slab"]
            state["slab"] += 1
            slot = n % NSLOT
            rows = scr[nm][s_ * 128:(s_ + 1) * 128, :]
            if nm != "diag" and (nm, s_) not in loaded_once:
                loaded_once.add((nm, s_))
                src = wsrc[nm][s_ * 128:(s_ + 1) * 128, :]
                S.add("pool", (lambda e, slot=slot, src=src, L=L: e.dma_start(out=ring[slot][:, 0:L], in_=src)),
                      writes=[("ring", slot)], dma_sem=ring_sem[slot])
                S.add("sp", (lambda e, slot=slot, rows=rows, L=L: e.dma_start(out=rows, in_=ring[slot][:, 0:L])),
                      reads=[("ring", slot)], writes=[("scr", nm, s_)], dma_sem=wb_sem[slot])
            else:
                S.add("sp", (lambda e, slot=slot, rows=rows, L=L: e.dma_start(out=ring[slot][:, 0:L], in_=rows)),
                      reads=[("scr", nm, s_)], writes=[("ring", slot)], dma_sem=ring_sem[slot])
            state["slabs_since"] += 1
            if state["pending_out"] is not None and state["slabs_since"] >= 3:
                state["pending_out"]()
                state["pending_out"] = None
            return slot

        def next_bank():
            b = state["bank"]
            state["bank"] = (b + 1) % 6
            return b

        def mm_group(bank, TW, pairs, reads, start=True, stop=True, c0=0, writes=None):
            def fn(e, bank=bank, TW=TW, pairs=pairs, start=start, stop=stop, c0=c0):
                ins = None
                n = len(pairs)
                for i_, (l, r) in enumerate(pairs):
                    ins = e.matmul(ps[:, bank, c0:c0 + TW], l, r, start=(start and i_ == 0), stop=(stop and i_ == n - 1))
                return ins
            return S.add("pe", fn, reads=reads, writes=([("ps", bank)] if writes is None else writes))

        def sq_slot_default(kc, TW):
            return hsq[:, kc % 2, 0:TW], ("hsq", kc % 2)

        def sq_slot_act(kc, TW):
            return act[:, 16 + kc, 0:TW], ("act", 16 + kc)

        def sq_slot_T(kc, TW):
            o = (kc // 2) * 2 * (TWM + 2) + (kc % 2) * TWM
            return Tb[:, o:o + TW], ("T", kc // 2)

        def rms_sq(hb, hi, TW, slotf, kcs=range(KD)):
            for kc in kcs:
                ap, key = slotf(kc, TW)
                S.add("act", (lambda e, kc=kc, ap=ap: e.activation(ap, hb[:, kc, 0:TW], AF.Square)),
                      reads=[("h", hi, kc)], writes=[key])

        def rms_stats(TW, slotf, mask_halo=False):
            for kc in range(KD):
                ap, key = slotf(kc, TW)

                def fn(e, kc=kc, ap=ap):
                    return e.matmul(ps[:, 7, 0:TW], meanD[:, :], ap, start=(kc == 0), stop=(kc == KD - 1))
                S.add("pe", fn, reads=[key, ("meanD",)], writes=[("ps", 7)])
            S.add("act", lambda e: e.activation(st_rstd[:, 0:TW], ps[:, 7, 0:TW], AF.Sqrt, bias=col(C_EPS), scale=1.0),
                  reads=[("ps", 7), ("csb",)], writes=[("rstd",)])
            S.add("dve", lambda e: e.reciprocal(st_rstd[:, 0:TW], st_rstd[:, 0:TW]),
                  reads=[("rstd",)], writes=[("rstd",)])
            if mask_halo:
                S.add("dve", lambda e: e.tensor_scalar(st_rstd[:, 0:HALO], st_rstd[:, 0:HALO], col(C_FLAG), None, ALU.mult),
                      reads=[("rstd",), ("csb",)], writes=[("rstd",)])

        def rms_apply(hb, hi, TW, gcol0, dst, dst_key, inplace=False):
            for kc in range(KD):
                S.add("dve", (lambda e, kc=kc: e.scalar_tensor_tensor(dst[:, kc, 0:TW], hb[:, kc, 0:TW], col(gcol0 + kc),
                                                                       st_rstd[:, 0:TW], ALU.mult, ALU.mult)),
                      reads=[("h", hi, kc), ("rstd",), ("csb",)], writes=[("h", hi, kc) if inplace else (dst_key, kc)])

        def rms(hb, hi, TW, gcol0, dst, dst_key, masked=False, inplace=False):
            for kc in range(KD):
                ap, key = sq_slot_default(kc, TW)
                S.add("act", (lambda e, kc=kc, ap=ap: e.activation(ap, hb[:, kc, 0:TW], AF.Square)),
                      reads=[("h", hi, kc)], writes=[key])

                def fn(e, kc=kc, ap=ap):
                    return e.matmul(ps[:, 7, 0:TW], meanD[:, :], ap, start=(kc == 0), stop=(kc == KD - 1))
                S.add("pe", fn, reads=[key, ("meanD",)], writes=[("ps", 7)])
            S.add("act", lambda e: e.activation(st_rstd[:, 0:TW], ps[:, 7, 0:TW], AF.Sqrt, bias=col(C_EPS), scale=1.0),
                  reads=[("ps", 7), ("csb",)], writes=[("rstd",)])
            S.add("dve", lambda e: e.reciprocal(st_rstd[:, 0:TW], st_rstd[:, 0:TW]),
                  reads=[("rstd",)], writes=[("rstd",)])
            rms_apply(hb, hi, TW, gcol0, dst, dst_key, inplace=inplace)

        def aconv_keys(g):
            return [("act", 16 + 2 * g), ("act", 17 + 2 * g)]

        def tile(q):
            hi = q % 2
            hb = h[hi]
            TW = WIDTHS[q]
            i = q
            AUX = "dve" if q == 0 else "pool"
            hn_all = [("hn", kc) for kc in range(KD)]
            if q == 0:
                rms(hb, hi, TW, C_G1, hn, "hn")

            stats_order = [g for g in range(GA) if g not in DVEG] + list(DVEG)
            g_first, g_last = stats_order[0], stats_order[-1]
            tapq = {g: [] for g in DVEG}
            rr = {"i": 0}

            def queue_taps(g):
                tapq[g].append(lambda: S.add("dve", (lambda e: e.tensor_scalar(aconv_t[:, g, 0:TW], abf[:, g, 0:TW], col(C_CAW + g * CA),
                                                                                col(C_CAB + g), ALU.mult, ALU.add)),
                                             reads=[("abf", g), ("csb",)], writes=aconv_keys(g)))
                for k in range(1, CA):
                    tapq[g].append(lambda k=k: S.add("dve", (lambda e: e.scalar_tensor_tensor(aconv_t[:, g, 0:TW], abf[:, g, k:k + TW],
                                                                                               col(C_CAW + g * CA + k),
                                                                                               aconv_t[:, g, 0:TW], ALU.mult, ALU.add)),
                                                     reads=[("abf", g), ("csb",)] + aconv_keys(g), writes=aconv_keys(g)))

            def drain(n):
                gs = [g for g in DVEG]
                while n > 0:
                    live = [g for g in gs if tapq[g]]
                    if not live:
                        return
                    g = live[rr["i"] % len(live)]
                    rr["i"] += 1
                    tapq[g].pop(0)()
                    n -= 1

            def stats_mm(g):
                S.add("pe", (lambda e: e.matmul(ps[:, 7, 0:TW], meanA[:, :], ab[:, g % 4, 0:TW], start=(g == g_first), stop=(g == g_last))),
                      reads=[("act", 32 + g % 4), ("meanA",)], writes=[("ps", 7)])
                S.add("pe", (lambda e: e.matmul(ps[:, 6, 0:TW], meanA[:, :], asq[:, g % 4, 0:TW], start=(g == g_first), stop=(g == g_last))),
                      reads=[("act", 36 + g % 4), ("meanA",)], writes=[("ps", 6)])

            pend_stats = []

            def flush_stats():
                while pend_stats:
                    stats_mm(pend_stats.pop(0))

            def conv_group(g):
                flush_stats()
                dsl = load_slab("diag", g, CA * 128)
                dg = ring[dsl][:, 0:CA * 128].rearrange("p (k m) -> p k m", m=128)
                bC = next_bank()
                mm_group(bC, TW, [(dg[:, k, :], abf[:, g, k:k + TW]) for k in range(CA)],
                         reads=[("ring", dsl), ("abf", g)])
                S.add("act", (lambda e: e.activation(aconv_t[:, g, 0:TW], ps[:, bC, 0:TW], AF.Identity,
                                                     bias=col(C_CAB + g), scale=1.0)),
                      reads=[("ps", bC), ("csb",)], writes=aconv_keys(g))
                S.add("act", (lambda e: e.activation(asq[:, g % 4, 0:TW], ps[:, bC, 0:TW], AF.Square,
                                                     bias=col(C_CAB + g), scale=1.0)),
                      reads=[("ps", bC), ("csb",)], writes=[("act", 36 + g % 4)])
                S.add("act", (lambda e: e.activation(ab[:, g % 4, 0:TW], ps[:, bC, 0:TW], AF.Identity,
                                                     bias=col(C_CAB + g), scale=1.0)),
                      reads=[("ps", bC), ("csb",)], writes=[("act", 32 + g % 4)])
                pend_stats.append(g)

            def sb_group_stats(g):
                S.add("act", (lambda e: e.activation(asq[:, g % 4, 0:TW], aconv_t[:, g, 0:TW], AF.Square)),
                      reads=aconv_keys(g), writes=[("act", 36 + g % 4)])
                S.add("act", (lambda e: e.activation(ab[:, g % 4, 0:TW], aconv_t[:, g, 0:TW], AF.Identity)),
                      reads=aconv_keys(g), writes=[("act", 32 + g % 4)])

            pend = None
            for g in range(GA):
                slot = load_slab("inA", g, 4096)
                rg = ring[slot][:, 0:4096].rearrange("p (k c) -> p k c", c=256)
                bV = next_bank()
                mm_group(bV, TW, [(rg[:, kc, 0:128], hn[:, kc, 0:TW]) for kc in range(KD)], reads=[("ring", slot)] + hn_all)
                bG = next_bank()
                mm_group(bG, TW, [(rg[:, kc, 128:256], hn[:, kc, 0:TW]) for kc in range(KD)], reads=[("ring", slot)] + hn_all)
                ts_ = g % 2
                S.add("act", (lambda e, bG=bG, ts_=ts_: e.activation(T[:, ts_, 0:TW], ps[:, bG, 0:TW], AF.Sigmoid)),
                      reads=[("ps", bG)], writes=[("T", ts_)])
                S.add("dve", (lambda e, bV=bV, ts_=ts_, g=g: e.tensor_tensor(abf[:, g, 30:30 + TW], ps[:, bV, 0:TW],
                                                                             T[:, ts_, 0:TW], ALU.mult)),
                      reads=[("ps", bV), ("T", ts_)], writes=[("abf", g)])
                if g < 4 and state.get("final_tail") is not None:
                    state["final_tail"](g)
                    if g == 3:
                        state["final_tail"] = None
                        state["slabs_since"] = 0
                drain(6 if g < 4 else 18)
                if pend is not None:
                    conv_group(pend)
                    pend = None
                if g in DVEG:
                    queue_taps(g)
                else:
                    pend = g
            if pend is not None:
                conv_group(pend)

            applyq = []

            def drain_apply(n):
                while n > 0 and applyq:
                    applyq.pop(0)()
                    n -= 1

            def ln_block():
                flush_stats()
                drain(10 ** 6)
                for g in DVEG:
                    sb_group_stats(g)
                for g in DVEG:
                    stats_mm(g)
                S.add("dve", lambda e: e.tensor_copy(abf[:, :, 0:30], abf[:, :, TW:TW + 30]),
                      reads=[("abf", g) for g in range(GA)], writes=[("abf", g) for g in range(GA)])
                S.add("act", lambda e: e.activation(st_mean[:, 0:TW], ps[:, 7, 0:TW], AF.Identity),
                      reads=[("ps", 7)], writes=[("mean",)])
                S.add("dve", lambda e: e.tensor_tensor(st_nmr[:, 0:TW], st_mean[:, 0:TW], st_mean[:, 0:TW], ALU.mult),
                      reads=[("mean",)], writes=[("nmr",)])
                S.add("dve", lambda e: e.tensor_tensor(st_var[:, 0:TW], ps[:, 6, 0:TW], st_nmr[:, 0:TW], ALU.subtract),
                      reads=[("ps", 6), ("nmr",)], writes=[("var",)])
                S.add("act", lambda e: e.activation(st_var[:, 0:TW], st_var[:, 0:TW], AF.Sqrt, bias=col(C_EPS), scale=1.0),
                      reads=[("var",), ("csb",)], writes=[("var",)])
                S.add("dve", lambda e: e.reciprocal(st_var[:, 0:TW], st_var[:, 0:TW]),
                      reads=[("var",)], writes=[("var",)])
                S.add("dve", lambda e: e.scalar_tensor_tensor(st_nmr[:, 0:TW], st_mean[:, 0:TW], -1.0, st_var[:, 0:TW],
                                                              ALU.mult, ALU.mult),
                      reads=[("mean",), ("var",)], writes=[("nmr",)])
                for g in range(GA):
                    def ap_(g=g):
                        S.add("dve", (lambda e: e.tensor_tensor(aconv_t[:, g, 0:TW], aconv_t[:, g, 0:TW], st_var[:, 0:TW], ALU.mult)),
                              reads=aconv_keys(g) + [("var",)], writes=aconv_keys(g))
                        S.add("dve", (lambda e: e.tensor_tensor(aconv_t[:, g, 0:TW], aconv_t[:, g, 0:TW], st_nmr[:, 0:TW], ALU.add)),
                              reads=aconv_keys(g) + [("nmr",)], writes=aconv_keys(g))
                        S.add("act", (lambda e: e.activation(cat[:, g, 0:TW], aconv_t[:, g, 0:TW], AF.Silu,
                                                             bias=col(C_LNB + g), scale=col(C_LNG + g))),
                              reads=aconv_keys(g) + [("csb",)], writes=[("act", g)])
                    applyq.append(ap_)

            for jp in range(4):
                if jp == LN_AT:
                    ln_block()
                for gi in range(2):
                    g = 2 * jp + gi
                    slot = load_slab("inB", 3 * jp + gi, 4096)
                    rg = ring[slot][:, 0:4096].rearrange("p (k c) -> p k c", c=256)
                    bC = next_bank()
                    mm_group(bC, TW, [(rg[:, kc, 0:128], hn[:, kc, 0:TW]) for kc in range(KD)], reads=[("ring", slot)] + hn_all)
                    bH = next_bank()
                    mm_group(bH, TW, [(rg[:, kc, 128:256], hn[:, kc, 0:TW]) for kc in range(KD)], reads=[("ring", slot)] + hn_all)
                    s2 = g % 2
                    S.add("act", (lambda e, bC=bC, s2=s2: e.activation(T[:, 2 + s2, 0:TW], ps[:, bC, 0:TW], AF.Identity)),
                          reads=[("ps", bC)], writes=[("T", 2 + s2)])
                    S.add("dve", (lambda e, g=g, s2=s2: e.tensor_copy(chb[:, s2, 0:2], chtail[:, g, :])),
                          reads=[("chtail", g)], writes=[("T", 6 + s2)])
                    S.add("dve", (lambda e, bH=bH, s2=s2: e.tensor_tensor(chb[:, s2, 2:2 + TW], ps[:, bH, 0:TW], T[:, 2 + s2, 0:TW], ALU.mult)),
                          reads=[("ps", bH), ("T", 2 + s2)], writes=[("T", 6 + s2)])
                    S.add("dve", (lambda e, g=g, s2=s2: e.tensor_copy(chtail[:, g, :], chb[:, s2, TW:TW + 2])),
                          reads=[("T", 6 + s2)], writes=[("chtail", g)])
                    S.add("dve", (lambda e, g=g, s2=s2: e.tensor_scalar(T[:, 4 + s2, 0:TW], chb[:, s2, 2:2 + TW], col(C_CBW + g * 3 + 2), None, ALU.mult)),
                          reads=[("T", 6 + s2), ("csb",)], writes=[("T", 4 + s2)])
                    for k in (1, 0):
                        S.add("dve", (lambda e, g=g, s2=s2, k=k: e.scalar_tensor_tensor(T[:, 4 + s2, 0:TW], chb[:, s2, k:k + TW],
                                                                                         col(C_CBW + g * 3 + k), T[:, 4 + s2, 0:TW],
                                                                                         ALU.mult, ALU.add)),
                              reads=[("T", 6 + s2), ("csb",), ("T", 4 + s2)], writes=[("T", 4 + s2)])
                    drain(TAPS_B)
                    drain_apply(3)
                if jp == LN_HALF:
                    ln_block()
                slot = load_slab("inB", 3 * jp + 2, 4096)
                rg = ring[slot][:, 0:4096].rearrange("p (k c) -> p k c", c=256)
                for gi in range(2):
                    g = 2 * jp + gi
                    s2 = g % 2
                    bB = next_bank()
                    mm_group(bB, TW, [(rg[:, kc, gi * 128:(gi + 1) * 128], hn[:, kc, 0:TW]) for kc in range(KD)], reads=[("ring", slot)] + hn_all)
                    S.add("dve", (lambda e, g=g, s2=s2, bB=bB: e.tensor_tensor(cat[:, GA + g, 0:TW], T[:, 4 + s2, 0:TW], ps[:, bB, 0:TW], ALU.mult)),
                          reads=[("T", 4 + s2), ("ps", bB)], writes=[("act", GA + g)])
                    drain(TAPS_B // 2)
                    drain_apply(3)
            drain_apply(100)
            cat_all = [("act", c) for c in range(KD)]
            for s_ in range(8):
                slot = load_slab("out", s_, 4096)
                rg = ring[slot][:, 0:4096].rearrange("p (k c) -> p k c", c=256)
                for t_ in range(2):
                    m = 2 * s_ + t_
                    b = next_bank()
                    if s_ == 0:
                        mm_group(b, TW, [(rg[:, kc, t_ * 128:(t_ + 1) * 128], cat[:, kc, 0:TW]) for kc in range(KD - 2)],
                                 reads=[("ring", slot)] + cat_all[:KD - 2], stop=False)
                        mm_group(b, TW, [(rg[:, kc, t_ * 128:(t_ + 1) * 128], cat[:, kc, 0:TW]) for kc in range(KD - 2, KD)],
                                 reads=[("ring", slot)] + cat_all[KD - 2:], start=False)
                    else:
                        mm_group(b, TW, [(rg[:, kc, t_ * 128:(t_ + 1) * 128], cat[:, kc, 0:TW]) for kc in range(KD)],
                                 reads=[("ring", slot)] + cat_all)
                    S.add("dve", (lambda e, m=m, b=b: e.tensor_tensor(hb[:, m, 0:TW], hb[:, m, 0:TW], ps[:, b, 0:TW], ALU.add)),
                          reads=[("h", hi, m), ("ps", b)], writes=[("h", hi, m)])
                    S.add("act", (lambda e, m=m: e.activation(hn[:, m, 0:TW], hb[:, m, 0:TW], AF.Identity, scale=col(C_G2 + m))),
                          reads=[("h", hi, m), ("csb",)], writes=[("hn", m)])
                    rms_sq(hb, hi, TW, sq_slot_T, kcs=[m])
            rms_stats(TW, sq_slot_T, mask_halo=(q == 0))
            if q + 1 < NT_RUN:
                load_x(q + 1)
            def ffn_restore(j):
                par = j % 2
                for w_ in range(2):
                    sl = 2 * j + w_
                    u = par * 2 + w_
                    S.add(AUX, (lambda e, u=u, sl=sl: e.tensor_copy(T[:, u, 0:2], utail[:, sl, :])),
                          reads=[("utail", sl)], writes=[("T", u)])

            ffn_restore(0)
            for j in range(KF):
                slot = load_slab("up", j, 4096)
                rg = ring[slot][:, 0:4096].rearrange("p (k c) -> p k c", c=256)
                par = j % 2
                banks = []
                for w_ in range(2):
                    b = next_bank()
                    banks.append(b)
                    mm_group(b, TW, [(rg[:, kc, w_ * 128:(w_ + 1) * 128], hn[:, kc, 0:TW]) for kc in range(KD)],
                             reads=[("ring", slot)] + hn_all)
                for w_ in range(2):
                    sl = 2 * j + w_
                    u = par * 2 + w_
                    b = banks[w_]
                    S.add("dve", (lambda e, u=u, b=b: e.tensor_tensor(T[:, u, 2:2 + TW], ps[:, b, 0:TW], st_rstd[:, 0:TW], ALU.mult)),
                          reads=[("ps", b), ("rstd",)], writes=[("T", u)])
                    S.add(AUX, (lambda e, u=u, sl=sl: e.tensor_copy(utail[:, sl, :], T[:, u, TW:TW + 2])),
                          reads=[("T", u)], writes=[("utail", sl)])
                    S.add("act", (lambda e, u=u, sl=sl: e.activation(T[:, 4 + u, 0:TW], T[:, u, 2:2 + TW], AF.Identity,
                                                                     scale=col(C_CFW + sl * 3 + 2))),
                          reads=[("T", u), ("csb",)], writes=[("T", 4 + u)])
                if j + 1 < KF:
                    ffn_restore(j + 1)
                for w_ in range(2):
                    sl = 2 * j + w_
                    u = par * 2 + w_
                    for k in (1, 0):
                        S.add("dve", (lambda e, u=u, sl=sl, k=k: e.scalar_tensor_tensor(T[:, 4 + u, 0:TW], T[:, u, k:k + TW],
                                                                                         col(C_CFW + sl * 3 + k), T[:, 4 + u, 0:TW],
                                                                                         ALU.mult, ALU.add)),
                              reads=[("T", u), ("T", u), ("csb",), ("T", 4 + u)], writes=[("T", 4 + u)])
                ug, uu = par * 2, par * 2 + 1
                S.add("act", (lambda e, ug=ug: e.activation(T[:, 4 + ug, 0:TW], T[:, 4 + ug, 0:TW], AF.Silu)),
                      reads=[("T", 4 + ug)], writes=[("T", 4 + ug)])
                S.add(AUX, (lambda e, ug=ug, uu=uu, j=j: e.tensor_tensor(act[:, j, 0:TW], T[:, 4 + ug, 0:TW], T[:, 4 + uu, 0:TW], ALU.mult)),
                      reads=[("T", 4 + ug), ("T", 4 + uu)], writes=[("act", j)])
            nxt = q + 1 < NT_RUN
            if nxt:
                hbn, hin, TWn = h[(q + 1) % 2], (q + 1) % 2, WIDTHS[q + 1]

                def sq_slot_hn(kc, TW_):
                    return hn[:, kc, 0:TW_], ("hn", kc)
                rms_sq(hbn, hin, TWn, sq_slot_hn)

            def down_half(m, hf, b):
                slot = load_slab("down", 2 * m + hf, 2816)
                rg = ring[slot][:, 0:2816].rearrange("p (f c) -> p f c", c=128)
                mm_group(b, TW, [(rg[:, f, :], act[:, hf * 22 + f, 0:TW]) for f in range(22)],
                         reads=[("ring", slot)] + [("act", hf * 22 + f) for f in range(22)],
                         start=(hf == 0), stop=(hf == 1))

            def hbf_slot(kc):
                o = (kc // 2) * 2 * (TWM + 2) + (kc % 2) * TWM
                return Tb[:, o:o + TW], ("T", kc // 2)

            def down_evac(m, b):
                S.add("dve", (lambda e, m=m, b=b: e.tensor_tensor(hb[:, m, 0:TW], hb[:, m, 0:TW], ps[:, b, 0:TW], ALU.add)),
                      reads=[("h", hi, m), ("ps", b)], writes=[("h", hi, m)])
                ap, key = hbf_slot(m)
                S.add("act", (lambda e, m=m, ap=ap: e.activation(ap, hb[:, m, 0:TW], AF.Identity)),
                      reads=[("h", hi, m)], writes=[key])

            b3 = [next_bank() for _ in range(3)]
            for m in range(3):
                down_half(m, 0, b3[m])
            for m in range(3):
                down_half(m, 1, b3[m])
                down_evac(m, b3[m])
            for m in range(3, KD):
                b = next_bank()
                down_half(m, 0, b)
                down_half(m, 1, b)
                down_evac(m, b)
            if nxt:
                rms_stats(TWn, sq_slot_hn)
            hbf_all = [("T", k) for k in range(8)]
            hsq32 = hsq[:, :, :].rearrange("p s t -> p (s t)").bitcast(F32)
            gsb_buf = [(st_mean, [("mean",)]), (st_var, [("var",)])]
            tpl_buf = [(st_nmr, [("nmr",)]), (hsq32, [("hsq", 0), ("hsq", 1)])]
            apply_q = []
            for s_ in range(8):
                slot = load_slab("gate", s_, 4096)
                rg = ring[slot][:, 0:4096].rearrange("p (k c) -> p k c", c=256)
                for t_ in range(2):
                    m = 2 * s_ + t_
                    bG = next_bank()
                    mm_group(bG, TW, [(rg[:, kc, t_ * 128:(t_ + 1) * 128], hbf_slot(kc)[0]) for kc in range(KD)],
                             reads=[("ring", slot)] + hbf_all)
                    bP = next_bank()
                    mm_group(bP, TW, [(plew[:, c, m * 128:(m + 1) * 128], pbf[:, i % 2, c, 0:TW]) for c in range(2)],
                             reads=[("plew",), ("pbf", i % 2)])
                    tg = m % 2
                    gsb, gk = gsb_buf[tg]
                    tpl, tk = tpl_buf[tg]
                    S.add("act", (lambda e, bG=bG, gsb=gsb, m=m: e.activation(gsb[:, 0:TW], ps[:, bG, 0:TW], AF.Sigmoid,
                                                                              bias=col(C_BG + m), scale=1.0)),
                          reads=[("ps", bG), ("csb",)], writes=gk)
                    S.add("dve", (lambda e, bP=bP, gsb=gsb, tpl=tpl: e.tensor_tensor(tpl[:, 0:TW], ps[:, bP, 0:TW], gsb[:, 0:TW], ALU.mult)),
                          reads=[("ps", bP)] + gk, writes=tk)
                    S.add(AUX, (lambda e, m=m, tpl=tpl: e.tensor_tensor(hb[:, m, 0:TW], hb[:, m, 0:TW], tpl[:, 0:TW], ALU.add)),
                          reads=[("h", hi, m)] + tk, writes=[("h", hi, m)])
                    rms_sq(hb, hi, TW, sq_slot_act, kcs=[m])
                    if nxt:
                        if 2 <= m <= 9:
                            for kc in (2 * (m - 2), 2 * (m - 2) + 1):
                                S.add("dve", (lambda e, kc=kc: e.scalar_tensor_tensor(hn[:, kc, 0:TWn], hbn[:, kc, 0:TWn], col(C_G1 + kc),
                                                                                       st_rstd[:, 0:TWn], ALU.mult, ALU.mult)),
                                      reads=[("h", hin, kc), ("rstd",), ("csb",)], writes=[("hn", kc)])
            def final_tail(part=None):
                if part is None or part == 0:
                    rms_stats(TW, sq_slot_act)
                kcs = range(KD) if part is None else range(4 * part, 4 * part + 4)
                for kc in kcs:
                    S.add("dve", (lambda e, kc=kc: e.scalar_tensor_tensor(hb[:, kc, 0:TW], hb[:, kc, 0:TW], col(C_GF + kc),
                                                                           st_rstd[:, 0:TW], ALU.mult, ALU.mult)),
                          reads=[("h", hi, kc), ("rstd",), ("csb",)], writes=[("h", hi, kc)])

            def emit_out():
                lo = HALO if q == 0 else 0
                y0 = C0S[q] + lo - HALO
                S.add("act", (lambda e: e.dma_start(out=yT[:, y0:y0 + TW - lo].rearrange("(k p) t -> p k t", p=128),
                                                    in_=hb[:, :, lo:TW])),
                      reads=[("h", hi, kc) for kc in range(KD)], dma_sem=o_sem[hi])
            if q == NT_RUN - 1:
                lo_ = HALO if q == 0 else 0
                y0_ = C0S[q] + lo_ - HALO
                for part in range(4):
                    final_tail(part)
                    S.add("act", (lambda e, part=part: e.dma_start(
                        out=yT[4 * part * 128:(4 * part + 4) * 128, y0_:y0_ + TW - lo_].rearrange("(k p) t -> p k t", p=128),
                        in_=hb[:, 4 * part:4 * part + 4, lo_:TW])),
                          reads=[("h", hi, kc) for kc in range(4 * part, 4 * part + 4)], dma_sem=o_sem[hi])
            else:
                state["final_tail"] = final_tail
                state["pending_out"] = emit_out
                state["slabs_since"] = -10 ** 6

        for q in range(NT_RUN):
            tile(q)
        if state["pending_out"] is not None:
            state["pending_out"]()
            state["pending_out"] = None

        sem_of = {}
        for e, lst in S.ops.items():
            cnt = 0
            for op in lst:
                if op.is_dma:
                    sem_of[id(op.sem)] = [op.sem, S.semcount[id(op.sem)]]
                elif op.sig:
                    cnt += 1
                    op.sig_val = cnt

        def emit(engname, eng):
            waited = {}
            for op in S.ops[engname]:
                need = {}
                for d in op.deps:
                    sm = d.sem if d.is_dma else esem[d.eng]
                    k = id(sm)
                    if k not in need or need[k][1] < d.sig_val:
                        need[k] = (sm, d.sig_val)
                for k, (sm, v) in need.items():
                    if waited.get(k, 0) >= v:
                        continue
                    eng.wait_ge(sm, v)
                    waited[k] = v
                ins = op.fn(eng)
                if op.is_dma:
                    ins.then_inc(op.sem, 16)
                elif op.sig:
                    ins.then_inc(esem[engname], 1)
            return waited

        with nc.Block() as block:
            @block.tensor
            def _(e):
                emit("pe", e)

            @block.scalar
            def _(e):
                emit("act", e)

            @block.vector
            def _(e):
                emit("dve", e)

            @block.gpsimd
            def _(e):
                emit("pool", e)

            @block.sync
            def _(e):
                emit("sp", e)
                for sm in o_sem:
                    c = sem_of.get(id(sm))
                    if c:
                        e.wait_ge(sm, c[1])
    return nc


def _slab(w, cols):
    K = w.shape[0] // 128
    sub = w[:, cols].reshape(K, 128, len(cols)).transpose(1, 0, 2)
    return np.ascontiguousarray(sub).reshape(128, K * len(cols))


def _prep_shared(inp):
    w_in = inp["w_in"][0]
    w_out = inp["w_out"][0]
    w_up = inp["w_up"][0]
    w_down = inp["w_down"][0]
    w_gate = inp["w_ple_gate"][0]
    w_ple = inp["w_ple_proj"][0]
    r128 = np.arange(128)
    inA = [_slab(w_in, np.concatenate([g * 128 + r128, 1024 + g * 128 + r128])) for g in range(8)]
    inB = []
    for jp in range(4):
        g0, g1 = 2 * jp, 2 * jp + 1
        inB.append(_slab(w_in, np.concatenate([3072 + g0 * 128 + r128, 4096 + g0 * 128 + r128])))
        inB.append(_slab(w_in, np.concatenate([3072 + g1 * 128 + r128, 4096 + g1 * 128 + r128])))
        inB.append(_slab(w_in, np.concatenate([2048 + g0 * 128 + r128, 2048 + g1 * 128 + r128])))
    outs = [_slab(w_out, np.arange(s * 256, (s + 1) * 256)) for s in range(8)]
    ups = [_slab(w_up, np.concatenate([j * 128 + r128, DFF + j * 128 + r128])) for j in range(KF)]
    downs = []
    for m in range(16):
        for hf in range(2):
            downs.append(_slab(w_down[hf * 22 * 128:(hf + 1) * 22 * 128, :], np.arange(m * 128, (m + 1) * 128)))
    gates = [_slab(w_gate, np.arange(s * 256, (s + 1) * 256)) for s in range(8)]
    ple = _slab(w_ple, np.arange(2048))
    sh = {
        "w_inA": np.concatenate(inA, 0), "w_inB": np.concatenate(inB, 0), "w_out": np.concatenate(outs, 0),
        "w_up": np.concatenate(ups, 0), "w_down": np.concatenate(downs, 0), "w_gate": np.concatenate(gates, 0),
        "w_ple": ple, "idn": np.eye(128, dtype=np.float32),
    }
    c = np.zeros((128, NCONST), np.float32)

    def chunked(v, n):
        return np.asarray(v).reshape(n, 128).T
    c[:, C_G1:C_G1 + 16] = chunked(inp["norm_mix_g"][0], 16)
    c[:, C_G2:C_G2 + 16] = chunked(inp["norm_ffn_g"][0], 16)
    c[:, C_GF:C_GF + 16] = chunked(inp["norm_final_g"], 16)
    c[:, C_CAB:C_CAB + 8] = chunked(inp["conv_a_b"][0], 8)
    c[:, C_LNG:C_LNG + 8] = chunked(inp["ln_a_g"][0], 8)
    c[:, C_LNB:C_LNB + 8] = chunked(inp["ln_a_b"][0], 8)
    cbw = inp["conv_b_w"][0]
    for g in range(8):
        for k in range(3):
            c[:, C_CBW + g * 3 + k] = cbw[k, g * 128:(g + 1) * 128]
    cfw = inp["conv_ffn_w"][0]
    for j in range(KF):
        for w_ in range(2):
            sl = 2 * j + w_
            f0 = (DFF if w_ else 0) + j * 128
            for k in range(3):
                c[:, C_CFW + sl * 3 + k] = cfw[k, f0:f0 + 128]
    c[:, C_BG:C_BG + 16] = chunked(inp["b_ple_gate"][0], 16)
    caw = inp["conv_a_w"][0]
    for g in range(8):
        for k in range(CA):
            c[:, C_CAW + g * CA + k] = caw[k, g * 128:(g + 1) * 128]
    return sh, c


_NC_CACHE = {}


def kernel(**inputs):
    inp = {k: np.asarray(v) for k, v in inputs.items()}
    x = inp["x"]
    p = inp["p"][0]
    sh, cbase = _prep_shared(inp)
    in_maps = []
    for c in range(NCORES):
        b, half = c // 2, c % 2
        t0 = half * TOK
        xt = np.zeros((D, HALO + TOK), np.float32)
        xt[:, HALO:] = x[b, t0:t0 + TOK, :].T
        if half == 1:
            xt[:, :HALO] = x[b, t0 - HALO:t0, :].T
        cst = cbase.copy()
        cst[:, C_FLAG] = 1.0 if half == 1 else 0.0
        cst[:, C_EPS] = EPS
        m = dict(sh)
        m["xT"] = xt
        pt = np.zeros((256, HALO + TOK), np.float32)
        pt[:, HALO:] = p[b, t0:t0 + TOK, :].T
        m["pT"] = pt
        m["cst"] = cst
        in_maps.append(m)
    if "nc" not in _NC_CACHE:
        _NC_CACHE["nc"] = build_program()
    nc = _NC_CACHE["nc"]
    res = run_bass_kernel_spmd(nc, in_maps, core_ids=list(range(NCORES)))
    out = np.empty((4, SEQ, D), np.float32)
    for c in range(NCORES):
        b, half = c // 2, c % 2
        out[b, half * TOK:(half + 1) * TOK, :] = np.asarray(res.results[c]["yT"]).T
    return out
```
